# Optimizing a Trainium2 kernel written in Bass

```python
import math
import jax
import jax.numpy as jnp
from jax import lax
import numpy as np

D_MODEL = 2048
BATCH = 2
SEQ = 8192
DEPTH = 4

N_MIXERS = 3
HEAD_DIM = 128
ROPE_THETA = 10000.0
NORM_EPS = 1e-6
D_FF = 4 * D_MODEL
BAND_BLOCK = 128

NSA_HEADS = D_MODEL // HEAD_DIM
NSA_KV_GROUPS = 4
NSA_REP = NSA_HEADS // NSA_KV_GROUPS
NSA_KV_WIDTH = NSA_KV_GROUPS * HEAD_DIM
CMP_BLOCK = 32
CMP_STRIDE = 16
CMP_HIDDEN = 4 * HEAD_DIM
SEL_BLOCK = 64
SEL_TOPK = 16
NSA_WINDOW = 512
NSA_Q_CHUNK = 64
FORCED_BONUS = 1e9
NSA_IN_WIDTH = NSA_HEADS * HEAD_DIM + 6 * NSA_KV_WIDTH + 3 * NSA_HEADS

DIL_HEADS = D_MODEL // HEAD_DIM
DIL_PATTERNS = ((128, 1), (512, 4), (2048, 16))
DIL_IN_WIDTH = len(DIL_PATTERNS) * 3 * DIL_HEADS * HEAD_DIM

D_RNN = 2688
RNN_BLOCKS = 16
RNN_BLOCK_DIM = D_RNN // RNN_BLOCKS
CONV_WIDTH = 4
LRU_C = 8.0

kernel_name = 'hybrid_nsa_dilated_rglru_adaln'

F32 = jnp.float32


def rmsnorm(x, gain):
    x32 = x.astype(F32)
    return x32 * lax.rsqrt(jnp.mean(x32 * x32, axis=-1, keepdims=True) + NORM_EPS) * gain.astype(F32)


def modulate(x, gain, shift, scale):
    return (rmsnorm(x, gain) * (1.0 + scale.astype(F32)) + shift.astype(F32)).astype(x.dtype)


def rope_tables(seq):
    inv_freq = ROPE_THETA ** (-jnp.arange(0, HEAD_DIM, 2, dtype=F32) / HEAD_DIM)
    ang = jnp.arange(seq, dtype=F32)[:, None] * inv_freq[None, :]
    return jnp.cos(ang), jnp.sin(ang)


def rope(t, cos, sin):
    t1, t2 = jnp.split(t, 2, axis=-1)
    return jnp.concatenate([t1 * cos - t2 * sin, t1 * sin + t2 * cos], axis=-1)


def banded_attention(q, k, v, window, block):
    q, k, v = q.astype(F32), k.astype(F32), v.astype(F32)
    N, G, R, L, dh = q.shape
    blk = min(block, L)
    nb = -(-L // blk)
    lp = nb * blk
    nw = -(-window // blk)
    q = jnp.pad(q, ((0, 0), (0, 0), (0, 0), (0, lp - L), (0, 0)))
    kv_pad = ((0, 0), (0, 0), (nw * blk, lp - L), (0, 0))
    kb = jnp.pad(k, kv_pad).reshape(N, G, nb + nw, blk, dh)
    vb = jnp.pad(v, kv_pad).reshape(N, G, nb + nw, blk, dh)
    kw = jnp.concatenate([kb[:, :, s:s + nb] for s in range(nw + 1)], axis=3)
    vw = jnp.concatenate([vb[:, :, s:s + nb] for s in range(nw + 1)], axis=3)
    qb = q.reshape(N, G, R, nb, blk, dh)
    s = jnp.einsum('ngrbqd,ngbkd->ngrbqk', qb, kw) / math.sqrt(dh)
    qpos = jnp.arange(nb)[:, None] * blk + jnp.arange(blk)[None, :]
    kpos = jnp.arange(nb)[:, None] * blk - nw * blk + jnp.arange((nw + 1) * blk)[None, :]
    diff = qpos[:, :, None] - kpos[:, None, :]
    mask = (diff >= 0) & (diff <= window) & (kpos[:, None, :] >= 0)
    s = jnp.where(mask, s, -jnp.inf)
    m = jnp.max(s, axis=-1, keepdims=True)
    p = jnp.exp(s - m)
    l = jnp.sum(p, axis=-1, keepdims=True)
    out = jnp.einsum('ngrbqk,ngbkd->ngrbqd', p, vw) / l
    lse = (m + jnp.log(l))[..., 0]
    out = out.reshape(N, G, R, lp, dh)[:, :, :, :L]
    lse = lse.reshape(N, G, R, lp)[:, :, :, :L]
    return out, lse


def nsa_mixer(h, w_in, cmp_pe, cmp_w1, cmp_w2, w_out, cos, sin):
    B, S, _ = h.shape
    G, R, dh = NSA_KV_GROUPS, NSA_REP, HEAD_DIM
    scale = 1.0 / math.sqrt(dh)
    hq = NSA_HEADS * dh
    proj = (h @ w_in).astype(F32)
    q = proj[..., :hq].reshape(B, S, G, R, dh).transpose(0, 2, 3, 1, 4)
    kv = proj[..., hq:hq + 6 * NSA_KV_WIDTH].reshape(B, S, 6, G, dh).transpose(2, 0, 3, 1, 4)
    k_cmp, v_cmp, k_sel, v_sel, k_win, v_win = kv[0], kv[1], kv[2], kv[3], kv[4], kv[5]
    gates = jax.nn.sigmoid(proj[..., hq + 6 * NSA_KV_WIDTH:]).reshape(B, S, G, R, 3)
    gates = gates.transpose(4, 0, 2, 3, 1)[..., None]
    q_rot = rope(q, cos, sin)
    k_sel = rope(k_sel, cos, sin)
    k_win = rope(k_win, cos, sin)

    ratio = CMP_BLOCK // CMP_STRIDE
    n_cmp = S // CMP_STRIDE - ratio + 1

    def compress(t, pe, w1, w2):
        pieces = t.reshape(B, G, S // CMP_STRIDE, CMP_STRIDE, dh)
        blocks = jnp.concatenate([pieces[:, :, j:j + n_cmp] for j in range(ratio)], axis=3) + pe.astype(F32)
        flat = blocks.reshape(B, G, n_cmp, CMP_BLOCK * dh)
        return jax.nn.gelu(flat @ w1.astype(F32)) @ w2.astype(F32)

    k_c = compress(k_cmp, cmp_pe[0], cmp_w1[0], cmp_w2[0])
    v_c = compress(v_cmp, cmp_pe[1], cmp_w1[1], cmp_w2[1])
    cmp_start = jnp.arange(n_cmp) * CMP_STRIDE
    cmp_end = cmp_start + CMP_BLOCK - 1

    n_sb = S // SEL_BLOCK
    n_top = min(SEL_TOPK, n_sb)
    sel_ids = jnp.arange(n_sb)
    sel_start = sel_ids * SEL_BLOCK
    overlap = ((cmp_start[:, None] < sel_start[None, :] + SEL_BLOCK)
               & (cmp_start[:, None] + CMP_BLOCK > sel_start[None, :])).astype(F32)
    k_sb = k_sel.reshape(B, G, n_sb, SEL_BLOCK, dh)
    v_sb = v_sel.reshape(B, G, n_sb, SEL_BLOCK, dh)
    gather_blocks = jax.vmap(jax.vmap(lambda blocks, ix: blocks[ix]))

    def chunk(ci):
        start = ci * NSA_Q_CHUNK
        qc = lax.dynamic_slice_in_dim(q, start, NSA_Q_CHUNK, axis=3)
        qrc = lax.dynamic_slice_in_dim(q_rot, start, NSA_Q_CHUNK, axis=3)
        t = start + jnp.arange(NSA_Q_CHUNK)
        s = jnp.einsum('bgrqd,bgnd->bgrqn', qc, k_c) * scale
        s = jnp.where(cmp_end[None, :] <= t[:, None], s, -jnp.inf)
        m = jnp.max(s, axis=-1, keepdims=True)
        m = jnp.where(jnp.isfinite(m), m, 0.0)
        p = jnp.exp(s - m)
        l = jnp.sum(p, axis=-1, keepdims=True)
        p = p / jnp.where(l > 0, l, 1.0)
        o_c = jnp.einsum('bgrqn,bgnd->bgrqd', p, v_c)
        imp = jnp.einsum('bgrqn,nj->bgqj', p, overlap)
        cur = t // SEL_BLOCK
        avail = sel_ids[None, :] <= cur[:, None]
        forced = (sel_ids[None, :] == 0) | (sel_ids[None, :] == cur[:, None]) | (sel_ids[None, :] == cur[:, None] - 1)
        score = jnp.where(avail, imp + FORCED_BONUS * forced, -jnp.inf)
        _, idx = lax.top_k(score, n_top)
        ks = gather_blocks(k_sb, idx)
        vs = gather_blocks(v_sb, idx)
        ss = jnp.einsum('bgrqd,bgqnkd->bgrqnk', qrc, ks) * scale
        kpos = idx[..., None] * SEL_BLOCK + jnp.arange(SEL_BLOCK)
        ok = kpos <= t[:, None, None]
        ss = jnp.where(ok[:, :, None], ss, -jnp.inf)
        ps = jax.nn.softmax(ss.reshape(ss.shape[:4] + (-1,)), axis=-1).reshape(ss.shape)
        o_s = jnp.einsum('bgrqnk,bgqnkd->bgrqd', ps, vs)
        return o_c, o_s

    o_cmp, o_sel = lax.map(chunk, jnp.arange(S // NSA_Q_CHUNK))
    o_cmp = jnp.moveaxis(o_cmp, 0, 3).reshape(B, G, R, S, dh)
    o_sel = jnp.moveaxis(o_sel, 0, 3).reshape(B, G, R, S, dh)
    o_win, _ = banded_attention(q_rot, k_win, v_win, NSA_WINDOW - 1, BAND_BLOCK)
    o = gates[0] * o_cmp + gates[1] * o_sel + gates[2] * o_win
    o = o.transpose(0, 3, 1, 2, 4).reshape(B, S, hq).astype(h.dtype)
    return o @ w_out


def dilated_mixer(h, w_in, w_out, cos, sin):
    B, S, _ = h.shape
    H, dh = DIL_HEADS, HEAD_DIM
    proj = (h @ w_in).astype(F32).reshape(B, S, len(DIL_PATTERNS), 3, H, dh).transpose(2, 3, 0, 4, 1, 5)
    outs, lses = [], []
    for g, (window, dil) in enumerate(DIL_PATTERNS):
        L = S // dil

        def to_residue(t):
            return t.reshape(B, H, L, dil, dh).transpose(0, 1, 3, 2, 4).reshape(B, H * dil, L, dh)

        q = to_residue(rope(proj[g, 0], cos, sin))[:, :, None]
        k = to_residue(rope(proj[g, 1], cos, sin))
        v = to_residue(proj[g, 2])
        o, lse = banded_attention(q, k, v, window // dil, BAND_BLOCK)
        outs.append(o.reshape(B, H, dil, L, dh).transpose(0, 1, 3, 2, 4).reshape(B, H, S, dh))
        lses.append(lse.reshape(B, H, dil, L).transpose(0, 1, 3, 2).reshape(B, H, S))
    wts = jax.nn.softmax(jnp.stack(lses), axis=0)[..., None]
    o = jnp.sum(wts * jnp.stack(outs), axis=0)
    o = o.transpose(0, 2, 1, 3).reshape(B, S, H * dh).astype(h.dtype)
    return o @ w_out


def rglru_mixer(h, w_in, conv_w, conv_b, w_gate, b_gate, lru_lambda, w_out):
    B, S, _ = h.shape
    proj = (h @ w_in).astype(F32)
    y = jax.nn.gelu(proj[..., :D_RNN])
    xr = proj[..., D_RNN:]
    xp = jnp.pad(xr, ((0, 0), (CONV_WIDTH - 1, 0), (0, 0)))
    x = conv_b.astype(F32) + xp[:, 0:S] * conv_w[0].astype(F32)
    for j in range(1, CONV_WIDTH):
        x = x + xp[:, j:j + S] * conv_w[j].astype(F32)
    xb = x.reshape(B, S, RNN_BLOCKS, RNN_BLOCK_DIM)
    gl = jnp.einsum('bsnc,gncd->gbsnd', xb, w_gate.astype(F32)).reshape(2, B, S, D_RNN)
    gl = gl + b_gate.astype(F32)[:, None, None, :]
    r = jax.nn.sigmoid(gl[0])
    i = jax.nn.sigmoid(gl[1])
    log_a = -LRU_C * r * jax.nn.softplus(-lru_lambda.astype(F32))
    a = jnp.exp(log_a)
    b = jnp.sqrt(-jnp.expm1(2.0 * log_a)) * (i * x)

    def combine(left, right):
        a1, b1 = left
        a2, b2 = right
        return a1 * a2, a2 * b1 + b2

    hs = lax.associative_scan(combine, (a, b), axis=1)[1]
    return (hs * y).astype(h.dtype) @ w_out


def sqrelu_mlp(h, w1, w2):
    return jnp.square(jax.nn.relu(h @ w1)) @ w2


def setup_inputs(seed: int = 0) -> dict:
    key = jax.random.key(seed)
    keys = iter(jax.random.split(key, 128))

    def normal(shape, scale):
        return jax.random.normal(next(keys), shape, F32) * scale

    D = D_MODEL
    inp = {'x': normal((BATCH, SEQ, D), 1.0), 'c': normal((BATCH, D), 1.0)}
    for li in range(DEPTH):
        p = 'l%d_' % li
        inp[p + 'w_ada'] = normal((D, 6 * D), 0.5 * D ** -0.5)
        inp[p + 'b_ada'] = normal((6 * D,), 0.02)
        inp[p + 'norm1'] = 1.0 + normal((D,), 0.02)
        kind = li % N_MIXERS
        if kind == 0:
            inp[p + 'w_in'] = normal((D, NSA_IN_WIDTH), D ** -0.5)
            inp[p + 'cmp_pe'] = normal((2, CMP_BLOCK, HEAD_DIM), 0.1)
            inp[p + 'cmp_w1'] = normal((2, CMP_BLOCK * HEAD_DIM, CMP_HIDDEN), (CMP_BLOCK * HEAD_DIM) ** -0.5)
            inp[p + 'cmp_w2'] = normal((2, CMP_HIDDEN, HEAD_DIM), CMP_HIDDEN ** -0.5)
            inp[p + 'w_out'] = normal((NSA_HEADS * HEAD_DIM, D), (NSA_HEADS * HEAD_DIM) ** -0.5)
        elif kind == 1:
            inp[p + 'w_in'] = normal((D, DIL_IN_WIDTH), D ** -0.5)
            inp[p + 'w_out'] = normal((DIL_HEADS * HEAD_DIM, D), (DIL_HEADS * HEAD_DIM) ** -0.5)
        else:
            inp[p + 'w_in'] = normal((D, 2 * D_RNN), D ** -0.5)
            inp[p + 'conv_w'] = normal((CONV_WIDTH, D_RNN), CONV_WIDTH ** -0.5)
            inp[p + 'conv_b'] = normal((D_RNN,), 0.02)
            inp[p + 'w_gate'] = normal((2, RNN_BLOCKS, RNN_BLOCK_DIM, RNN_BLOCK_DIM), RNN_BLOCK_DIM ** -0.5)
            inp[p + 'b_gate'] = normal((2, D_RNN), 0.02)
            u = jax.random.uniform(next(keys), (D_RNN,), F32, minval=0.9, maxval=0.999)
            a0 = u ** (1.0 / LRU_C)
            inp[p + 'lambda'] = jnp.log(a0) - jnp.log1p(-a0)
            inp[p + 'w_out'] = normal((D_RNN, D), D_RNN ** -0.5)
        inp[p + 'norm2'] = 1.0 + normal((D,), 0.02)
        inp[p + 'w_ff1'] = normal((D, D_FF), D ** -0.5)
        inp[p + 'w_ff2'] = normal((D_FF, D), D_FF ** -0.5)
    inp['norm_f'] = 1.0 + normal((D,), 0.02)
    return inp


def reference(x, c,
              l0_w_ada, l0_b_ada, l0_norm1, l0_w_in, l0_cmp_pe, l0_cmp_w1, l0_cmp_w2, l0_w_out, l0_norm2, l0_w_ff1, l0_w_ff2,
              l1_w_ada, l1_b_ada, l1_norm1, l1_w_in, l1_w_out, l1_norm2, l1_w_ff1, l1_w_ff2,
              l2_w_ada, l2_b_ada, l2_norm1, l2_w_in, l2_conv_w, l2_conv_b, l2_w_gate, l2_b_gate, l2_lambda, l2_w_out, l2_norm2, l2_w_ff1, l2_w_ff2,
              l3_w_ada, l3_b_ada, l3_norm1, l3_w_in, l3_cmp_pe, l3_cmp_w1, l3_cmp_w2, l3_w_out, l3_norm2, l3_w_ff1, l3_w_ff2,
              norm_f):
    S = x.shape[1]
    cos, sin = rope_tables(S)
    layers = (
        (l0_w_ada, l0_b_ada, l0_norm1, l0_norm2, l0_w_ff1, l0_w_ff2, (l0_w_in, l0_cmp_pe, l0_cmp_w1, l0_cmp_w2, l0_w_out)),
        (l1_w_ada, l1_b_ada, l1_norm1, l1_norm2, l1_w_ff1, l1_w_ff2, (l1_w_in, l1_w_out)),
        (l2_w_ada, l2_b_ada, l2_norm1, l2_norm2, l2_w_ff1, l2_w_ff2,
         (l2_w_in, l2_conv_w, l2_conv_b, l2_w_gate, l2_b_gate, l2_lambda, l2_w_out)),
        (l3_w_ada, l3_b_ada, l3_norm1, l3_norm2, l3_w_ff1, l3_w_ff2, (l3_w_in, l3_cmp_pe, l3_cmp_w1, l3_cmp_w2, l3_w_out)),
    )
    cond = jax.nn.silu(c)
    for li in range(DEPTH):
        w_ada, b_ada, n1, n2, ff1, ff2, mix = layers[li]
        mod = (cond @ w_ada + b_ada)[:, None, :]
        sh1, sc1, g1, sh2, sc2, g2 = jnp.split(mod, 6, axis=-1)
        hn = modulate(x, n1, sh1, sc1)
        kind = li % N_MIXERS
        if kind == 0:
            y = nsa_mixer(hn, *mix, cos, sin)
        elif kind == 1:
            y = dilated_mixer(hn, *mix, cos, sin)
        else:
            y = rglru_mixer(hn, *mix)
        x = x + g1 * y
        x = x + g2 * sqrelu_mlp(modulate(x, n2, sh2, sc2), ff1, ff2)
    return rmsnorm(x, norm_f).astype(x.dtype)
```

```python
import numpy as np
from contextlib import ExitStack
import ml_dtypes
import concourse.bass as bass
import concourse.mybir as mybir
from concourse.bass_utils import run_bass_kernel_spmd

F32 = mybir.dt.float32
BF16 = mybir.dt.bfloat16
ALU = mybir.AluOpType
AF = mybir.ActivationFunctionType
AX = mybir.AxisListType
NPBF = ml_dtypes.bfloat16

D = 2048
S = 8192
B = 2
NCORES = 8
TPC = 2048
DFF = 8192
EPS = 1e-6
HD = 128
SCALE = 1.0 / float(np.sqrt(128.0))


class Buf:
    __slots__ = ("t", "w", "r", "dsem", "dcnt", "name", "psum")

    def __init__(self, t, name):
        self.t = t
        self.name = name
        self.w = []
        self.r = []
        self.dsem = None
        self.dcnt = 0
        self.psum = False

    def __getitem__(self, k):
        return self.t[k]


class Ctx:
    def __init__(self):
        self.nc = bass.Bass("TRN2", target_bir_lowering=False)
        self.es = ExitStack()
        nc = self.nc
        self.eng = {"pe": nc.tensor, "dve": nc.vector, "act": nc.scalar, "pool": nc.gpsimd, "sp": nc.sync}
        self.csem = {}
        self.ccnt = {}
        for k in ("pe", "dve", "act", "pool"):
            self.csem[k] = self.es.enter_context(nc.semaphore("c_" + k))
            self.ccnt[k] = 0
        self.seen = {q: {} for q in self.eng}
        self.dma_bufs = []
        self.n_ins = 0

    def din(self, name, shape, dt):
        return self.nc.dram_tensor(name, list(shape), dt, kind="ExternalInput").ap()

    def dout(self, name, shape, dt):
        return self.nc.dram_tensor(name, list(shape), dt, kind="ExternalOutput").ap()

    def sbuf(self, name, shape, dt):
        return Buf(self.es.enter_context(self.nc.sbuf_tensor(name, list(shape), dt)), name)

    def psum(self, name, shape, dt=F32):
        b = Buf(self.es.enter_context(self.nc.psum_tensor(name, list(shape), dt)), name)
        b.psum = True
        return b

    def _wait(self, q, toks, war_toks=()):
        e = self.eng[q]
        for (key, sem, val) in toks:
            if key == q and q == "pe":
                continue
            if self.seen[q].get(key, 0) < val:
                e.wait_ge(sem, val)
                self.seen[q][key] = val
        for (key, sem, val) in war_toks:
            if key == q:
                continue
            if self.seen[q].get(key, 0) < val:
                e.wait_ge(sem, val)
                self.seen[q][key] = val

    @staticmethod
    def _add_reader(b, tok):
        b.r = [t for t in b.r if t[0] != tok[0]] + [tok]

    def op(self, q, fn, reads=(), writes=()):
        raw = []
        war = []
        for b in reads:
            raw += b.w
            if b.psum:
                war += b.r
        for b in writes:
            raw += b.w
            war += b.r
        self._wait(q, raw, war)
        ins = fn(self.eng[q])
        self.ccnt[q] += 1
        ins.then_inc(self.csem[q], 1)
        self.n_ins += 1
        tok = (q, self.csem[q], self.ccnt[q])
        for b in writes:
            b.w = [tok]
            b.r = []
        for b in reads:
            if b not in writes:
                self._add_reader(b, tok)
        return ins

    def dma(self, q, out_ap, in_ap, own, reads=(), writes=()):
        cls = "sw" if q == "pool" else "hw"
        if own.dsem is None:
            own.dsem = {}
            own.dcnt = {}
        if cls not in own.dsem:
            own.dsem[cls] = self.es.enter_context(self.nc.semaphore("d%s_%s" % (cls, own.name)))
            own.dcnt[cls] = 0
            self.dma_bufs.append((own, cls))
        key = "d%s_%s" % (cls, own.name)
        raw = []
        war = []
        for b in reads:
            raw += [t for t in b.w if t[0] != key]
        for b in writes:
            raw += [t for t in b.w if t[0] != key]
            war += [t for t in b.r if t[0] != key]
        self._wait(q, raw, war)
        ins = self.eng[q].dma_start(out=out_ap, in_=in_ap)
        own.dcnt[cls] += 16
        ins.then_inc(own.dsem[cls], 16)
        self.n_ins += 1
        tok = (key, own.dsem[cls], own.dcnt[cls])
        for b in writes:
            if b.w and b.w[0][0] == key:
                b.w = [tok]
            elif b.w and b.w[0][0].startswith("d") and b.w[0][0].endswith("_" + own.name) and b is own:
                b.w = [t for t in b.w if t[0] != key] + [tok]
            else:
                b.w = [tok]
            b.r = []
        for b in reads:
            self._add_reader(b, tok)
        return ins

    def finish(self):
        for (b, cls) in self.dma_bufs:
            self.eng["sp"].wait_ge(b.dsem[cls], b.dcnt[cls])
        self.es.close()
        return self.nc


def make_ident(cx):
    ident = cx.sbuf("ident", [128, 128], BF16)
    cx.op("pool", lambda e: e.memset(ident[:], 0.0), writes=[ident])
    cx.op("pool", lambda e: e.affine_select(out=ident[:], in_=ident[:], pattern=[[-1, 128]],
                                            compare_op=ALU.not_equal, fill=1.0, base=0,
                                            channel_multiplier=1), reads=[ident], writes=[ident])
    return ident


def load_bcast(cx, q, dst, vec_ap, n):
    cx.dma(q, dst[:, 0:n], vec_ap.partition_broadcast(128), own=dst, writes=[dst])


class NormMod:
    def __init__(self, cx, ident, tag):
        self.cx = cx
        self.ident = ident
        self.A = cx.sbuf("nmA" + tag, [128, D], F32)
        self.Bv = cx.sbuf("nmB" + tag, [128, D], F32)
        self.sq = cx.sbuf("nmsq" + tag, [128, D], BF16)
        self.ss = cx.sbuf("nmss" + tag, [128, 1], F32)
        self.rs = cx.sbuf("nmrs" + tag, [128, 1], F32)
        self.hn = cx.sbuf("nmhn" + tag, [128, D], BF16)
        self.tmp = cx.sbuf("nmtmp" + tag, [128, D], F32)
        self.pT = [cx.psum("nmpT%d" % i + tag, [128, 4, 128], BF16) for i in range(2)]
        self.k = 0

    def load_params(self, gain_ap, scale_ap, shift_ap):
        cx = self.cx
        load_bcast(cx, "sp", self.A, scale_ap, D)
        load_bcast(cx, "sp", self.tmp, gain_ap, D)
        load_bcast(cx, "sp", self.Bv, shift_ap, D)
        cx.op("dve", lambda e: e.scalar_tensor_tensor(out=self.A[:], in0=self.A[:], scalar=1.0, in1=self.tmp[:],
                                                      op0=ALU.add, op1=ALU.mult),
              reads=[self.A, self.tmp], writes=[self.A])

    def load_gain_only(self, gain_ap):
        cx = self.cx
        load_bcast(cx, "sp", self.A, gain_ap, D)

    def stats(self, xt, xap):
        cx = self.cx
        cx.op("act", lambda e: e.activation(out=self.sq[:], in_=xap, func=AF.Square, accum_out=self.ss[:]),
              reads=[xt], writes=[self.sq, self.ss])
        cx.op("dve", lambda e: e.tensor_scalar(out=self.rs[:], in0=self.ss[:], scalar1=1.0 / D, scalar2=EPS,
                                               op0=ALU.mult, op1=ALU.add), reads=[self.ss], writes=[self.rs])
        cx.op("act", lambda e: e.activation(out=self.rs[:], in_=self.rs[:], func=AF.Sqrt),
              reads=[self.rs], writes=[self.rs])
        cx.op("dve", lambda e: e.reciprocal(out=self.rs[:], in_=self.rs[:]), reads=[self.rs], writes=[self.rs])

    def apply_T(self, xt, xap, dstT, dst_ap_fn):
        cx = self.cx
        self.stats(xt, xap)
        cx.op("dve", lambda e: e.scalar_tensor_tensor(out=self.tmp[:], in0=xap, scalar=self.rs[:, 0:1],
                                                      in1=self.A[:], op0=ALU.mult, op1=ALU.mult),
              reads=[xt, self.rs, self.A], writes=[self.tmp])
        cx.op("dve", lambda e: e.tensor_tensor(out=self.hn[:], in0=self.tmp[:], in1=self.Bv[:], op=ALU.add),
              reads=[self.tmp, self.Bv], writes=[self.hn])
        for g in range(4):
            pT = self.pT[self.k % 2]
            self.k += 1
            for j in range(4):
                kk = g * 4 + j
                cx.op("pe", lambda e: e.transpose(out=pT[:, j, :], in_=self.hn[:, kk * 128:(kk + 1) * 128],
                                                  identity=self.ident[:]),
                      reads=[self.hn, self.ident], writes=[pT])
            cx.op("act", lambda e: e.copy(out=dst_ap_fn(g), in_=pT[:]), reads=[pT], writes=[dstT])


ADA_COLS = 6 * D // NCORES


def build_ada():
    cx = Ctx()
    cT = cx.din("cT", [128, 16, 2], F32)
    wada = cx.din("wada", [4, D, ADA_COLS], F32)
    bada = cx.din("bada", [4, 2, ADA_COLS], F32)
    mod = cx.dout("mod", [4, 2, ADA_COLS], F32)
    ct = cx.sbuf("ct", [128, 16, 2], F32)
    bt = cx.sbuf("bt", [2, 4, ADA_COLS], F32)
    slabs = [cx.sbuf("aw%d" % i, [128, 16, 512], F32) for i in range(2)]
    ps = [cx.psum("aps%d" % i, [2, 512]) for i in range(2)]
    ot = [cx.sbuf("aot%d" % i, [2, 512], F32) for i in range(2)]
    cx.dma("sp", ct[:], cT, own=ct, writes=[ct])
    cx.dma("sp", bt[:], bada.rearrange("l b n -> b l n"), own=bt, writes=[bt])
    cx.op("act", lambda e: e.activation(out=ct[:], in_=ct[:], func=AF.Silu), reads=[ct], writes=[ct])
    i = 0
    for l in range(4):
        for cg in range(ADA_COLS // 512):
            sl = slabs[i % 2]
            p = ps[i % 2]
            o = ot[i % 2]
            cx.dma("sp" if i % 2 == 0 else "act", sl[:],
                   wada[l, :, cg * 512:(cg + 1) * 512].rearrange("(k p) n -> p k n", p=128), own=sl, writes=[sl])
            for k in range(16):
                cx.op("pe", lambda e: e.matmul(p[:], lhsT=ct[:, k, :], rhs=sl[:, k, :], start=(k == 0), stop=(k == 15)),
                      reads=[ct, sl], writes=[p])
            cx.op("dve", lambda e: e.tensor_tensor(out=o[:], in0=p[:], in1=bt[:, l, cg * 512:(cg + 1) * 512], op=ALU.add),
                  reads=[p, bt], writes=[o])
            cx.dma("sp", mod[l, :, cg * 512:(cg + 1) * 512], o[:], own=o, reads=[o])
            i += 1
    return cx.finish()


def run_ada(inp):
    nc = build_ada()
    c = inp["c"]
    cT = np.ascontiguousarray(c.T.reshape(16, 128, 2).transpose(1, 0, 2))
    maps = []
    for core in range(NCORES):
        sl = slice(core * ADA_COLS, (core + 1) * ADA_COLS)
        wada = np.stack([np.ascontiguousarray(inp["l%d_w_ada" % l][:, sl]) for l in range(4)])
        bada = np.stack([np.broadcast_to(inp["l%d_b_ada" % l][sl], (2, ADA_COLS)) for l in range(4)])
        maps.append({"cT": cT, "wada": wada, "bada": np.ascontiguousarray(bada)})
    res = run_bass_kernel_spmd(nc, maps, core_ids=list(range(NCORES)))
    return np.concatenate([r["mod"] for r in res.results], axis=2)


def build_pre():
    cx = Ctx()
    x = cx.din("x", [TPC, D], F32)
    gain = cx.din("gain", [D], F32)
    scale = cx.din("scale", [D], F32)
    shift = cx.din("shift", [D], F32)
    hT = cx.dout("hT", [D, TPC], BF16)
    ident = make_ident(cx)
    nm = NormMod(cx, ident, "p")
    nm.load_params(gain, scale, shift)
    xt = [cx.sbuf("xt%d" % i, [128, D], F32) for i in range(2)]
    hTs = [cx.sbuf("hTs%d" % i, [128, 16, 512], BF16) for i in range(2)]
    hTd = hT.rearrange("(k p) t -> p k t", p=128)
    for i in range(TPC // 128):
        xb = xt[i % 2]
        blk, ti = divmod(i, 4)
        hb = hTs[blk % 2]
        cx.dma("sp", xb[:], x[i * 128:(i + 1) * 128, :], own=xb, writes=[xb])
        nm.apply_T(xb, xb[:], hb, lambda g: hb[:, 4 * g:4 * g + 4, ti * 128:(ti + 1) * 128])
        if ti == 3:
            cx.dma("act", hTd[:, :, blk * 512:(blk + 1) * 512], hb[:], own=hb, reads=[hb])
    return cx.finish()


def build_post(KO, last):
    KOC = KO // 128
    NG = TPC // 512
    cx = Ctx()
    x = cx.din("x", [TPC, D], F32)
    oT = cx.din("oT", [KO, TPC], BF16)
    w_out = cx.din("w_out", [KO, D], F32)
    g1 = cx.din("g1", [D], F32)
    gain2 = cx.din("gain2", [D], F32)
    sc2 = cx.din("sc2", [D], F32)
    sh2 = cx.din("sh2", [D], F32)
    g2 = cx.din("g2", [D], F32)
    ff1 = cx.din("ff1", [D, DFF], F32)
    ff2 = cx.din("ff2", [DFF, D], F32)
    normf = cx.din("normf", [D], F32) if last else None
    xo = cx.dout("xo", [TPC, D], F32)

    ident = make_ident(cx)
    nm = NormMod(cx, ident, "q")
    nm.load_params(gain2, sc2, sh2)
    G1 = cx.sbuf("G1", [128, D], F32)
    G2 = cx.sbuf("G2", [128, D], F32)
    load_bcast(cx, "act", G1, g1, D)
    load_bcast(cx, "act", G2, g2, D)
    if last:
        NF = cx.sbuf("NF", [128, D], F32)
        load_bcast(cx, "act", NF, normf, D)
    xg = cx.sbuf("xg", [128, 4, D], F32)
    actT = cx.sbuf("actT", [128, max(KOC, 16), 512], BF16)
    h1T = cx.sbuf("h1T", [128, 4, 512], BF16)
    rl = [cx.sbuf("rl%d" % i, [128, 512], F32) for i in range(2)]
    tmpe = [cx.sbuf("tmpe%d" % i, [128, 512], F32) for i in range(2)]
    NSL = 4
    slabs = [cx.sbuf("slab%d" % i, [128, 16, 512], BF16) for i in range(NSL)]
    psA = [cx.psum("psA%d" % i, [128, 512]) for i in range(2)]
    psB = [cx.psum("psB%d" % i, [128, 512]) for i in range(2)]
    cnt = {"s": 0, "a": 0, "b": 0}

    def next_slab():
        s = slabs[cnt["s"] % NSL]
        cnt["s"] += 1
        return s

    oTd = oT.rearrange("(k p) t -> p k t", p=128)
    for grp in range(NG):
        t0 = grp * 512
        for ti in range(4):
            cx.dma("sp", xg[:, ti, :], x[t0 + ti * 128:t0 + (ti + 1) * 128, :], own=xg, writes=[xg])
        cx.dma("act", actT[:, 0:KOC, :], oTd[:, :, t0:t0 + 512], own=actT, writes=[actT])
        for cg in range(4):
            parts = []
            k0 = 0
            while k0 < KOC:
                kn = min(16, KOC - k0)
                sl = next_slab()
                cx.dma("pool", sl[:, 0:kn, :],
                       w_out[k0 * 128:(k0 + kn) * 128, cg * 512:(cg + 1) * 512].rearrange("(k p) n -> p k n", p=128),
                       own=sl, writes=[sl])
                parts.append((sl, k0, kn))
                k0 += kn
            for ti in range(4):
                pb = psB[cnt["b"] % 2]
                te = tmpe[cnt["b"] % 2]
                cnt["b"] += 1
                for (sl, k0, kn) in parts:
                    for kk in range(kn):
                        k = k0 + kk
                        cx.op("pe", lambda e: e.matmul(pb[:], lhsT=actT[:, k, ti * 128:(ti + 1) * 128], rhs=sl[:, kk, :],
                                                       start=(k == 0), stop=(k == KOC - 1)),
                              reads=[actT, sl], writes=[pb])
                cx.op("dve", lambda e: e.tensor_tensor(out=te[:], in0=pb[:], in1=G1[:, cg * 512:(cg + 1) * 512], op=ALU.mult),
                      reads=[pb, G1], writes=[te])
                cx.op("dve", lambda e: e.tensor_tensor(out=xg[:, ti, cg * 512:(cg + 1) * 512],
                                                       in0=xg[:, ti, cg * 512:(cg + 1) * 512], in1=te[:], op=ALU.add),
                      reads=[te, xg], writes=[xg])
        for ti in range(4):
            nm.apply_T(xg, xg[:, ti, :], actT, lambda g: actT[:, 4 * g:4 * g + 4, ti * 128:(ti + 1) * 128])
        for hb in range(DFF // 512):
            s1 = next_slab()
            cx.dma("pool", s1[:], ff1[:, hb * 512:(hb + 1) * 512].rearrange("(k p) n -> p k n", p=128), own=s1, writes=[s1])
            s2 = next_slab()
            s2f = s2[:].rearrange("p a b -> p (a b)")
            cx.dma("pool", s2[:].rearrange("p (j a) b -> p j (a b)", j=4),
                   ff2[hb * 512:(hb + 1) * 512, :].rearrange("(j p) n -> p j n", p=128), own=s2, writes=[s2])
            for jj in range(4):
                pa = psA[cnt["a"] % 2]
                r = rl[cnt["a"] % 2]
                cnt["a"] += 1
                for k in range(16):
                    cx.op("pe", lambda e: e.matmul(pa[:], lhsT=s1[:, k, jj * 128:(jj + 1) * 128], rhs=actT[:, k, :],
                                                   start=(k == 0), stop=(k == 15)), reads=[s1, actT], writes=[pa])
                cx.op("act", lambda e: e.activation(out=r[:], in_=pa[:], func=AF.Relu), reads=[pa], writes=[r])
                cx.op("act", lambda e: e.activation(out=h1T[:, jj, :], in_=r[:], func=AF.Square), reads=[r], writes=[h1T])
            for ti in range(4):
                for cg in range(4):
                    pb = psB[cnt["b"] % 2]
                    te = tmpe[cnt["b"] % 2]
                    cnt["b"] += 1
                    for jj in range(4):
                        cx.op("pe", lambda e: e.matmul(pb[:], lhsT=h1T[:, jj, ti * 128:(ti + 1) * 128],
                                                       rhs=s2f[:, jj * 2048 + cg * 512:jj * 2048 + (cg + 1) * 512],
                                                       start=(jj == 0), stop=(jj == 3)), reads=[h1T, s2], writes=[pb])
                    cx.op("dve", lambda e: e.tensor_tensor(out=te[:], in0=pb[:], in1=G2[:, cg * 512:(cg + 1) * 512], op=ALU.mult),
                          reads=[pb, G2], writes=[te])
                    cx.op("dve", lambda e: e.tensor_tensor(out=xg[:, ti, cg * 512:(cg + 1) * 512],
                                                           in0=xg[:, ti, cg * 512:(cg + 1) * 512], in1=te[:], op=ALU.add),
                          reads=[te, xg], writes=[xg])
        for ti in range(4):
            if last:
                nm.stats(xg, xg[:, ti, :])
                cx.op("dve", lambda e: e.scalar_tensor_tensor(out=xg[:, ti, :], in0=xg[:, ti, :], scalar=nm.rs[:, 0:1],
                                                              in1=NF[:], op0=ALU.mult, op1=ALU.mult),
                      reads=[xg, nm.rs, NF], writes=[xg])
            cx.dma("sp", xo[t0 + ti * 128:t0 + (ti + 1) * 128, :], xg[:, ti, :], own=xg, reads=[xg])
    return cx.finish()


RCH = 672
RP = 84
RNC = 8
GELU_C = 1.5957691216057308


def build_rglru():
    cx = Ctx()
    hT = cx.din("hT", [D, S], BF16)
    w_y = cx.din("w_y", [D, RCH], F32)
    w_x = cx.din("w_x", [D, RCH], F32)
    convw = cx.din("convw", [RP, RNC, 4], F32)
    convb = cx.din("convb", [RP, RNC], F32)
    bgate = cx.din("bgate", [RP, 2, RNC], F32)
    lam = cx.din("lam", [RP, RNC], F32)
    wg = cx.din("wg", [RP, 2 * 4 * 2, 168], F32)
    oT = cx.dout("oT", [RCH, S], BF16)

    Wy = cx.sbuf("Wy", [128, 16, RCH], BF16)
    Wx = cx.sbuf("Wx", [128, 16, RCH], BF16)
    WG = cx.sbuf("WG", [RP, 16, 168], BF16)
    cw = cx.sbuf("cw", [RP, RNC, 4], F32)
    cb = cx.sbuf("cb", [RP, RNC], F32)
    bg = cx.sbuf("bg", [RP, 2, RNC], F32)
    cneg = cx.sbuf("cneg", [RP, RNC], F32)
    cneg2 = cx.sbuf("cneg2", [RP, RNC], F32)
    ST = cx.sbuf("ST", [RP, RNC], F32)
    XR = cx.sbuf("XR", [RP, RNC, 515], F32)
    cx.dma("pool", Wy[:], w_y.rearrange("(k p) n -> p k n", p=128), own=Wy, writes=[Wy])
    cx.dma("pool", Wx[:], w_x.rearrange("(k p) n -> p k n", p=128), own=Wx, writes=[Wx])
    cx.dma("pool", WG[:], wg, own=WG, writes=[WG])
    cx.dma("sp", cw[:], convw, own=cw, writes=[cw])
    cx.dma("sp", cb[:], convb, own=cb, writes=[cb])
    cx.dma("sp", bg[:], bgate, own=bg, writes=[bg])
    cx.dma("sp", cneg[:], lam, own=cneg, writes=[cneg])
    cx.op("act", lambda e: e.activation(out=cneg[:], in_=cneg[:], func=AF.Exp, scale=-1.0), reads=[cneg], writes=[cneg])
    cx.op("act", lambda e: e.activation(out=cneg[:], in_=cneg[:], func=AF.Ln, bias=1.0), reads=[cneg], writes=[cneg])
    cx.op("dve", lambda e: e.tensor_scalar(out=cneg[:], in0=cneg[:], scalar1=-8.0, scalar2=None, op0=ALU.mult),
          reads=[cneg], writes=[cneg])
    cx.op("dve", lambda e: e.tensor_scalar(out=cneg2[:], in0=cneg[:], scalar1=2.0, scalar2=None, op0=ALU.mult),
          reads=[cneg], writes=[cneg2])
    cx.op("dve", lambda e: e.memset(ST[:], 0.0), writes=[ST])
    cx.op("dve", lambda e: e.memset(XR[:], 0.0), writes=[XR])

    hTb = [cx.sbuf("hTb%d" % i, [128, 16, 512], BF16) for i in range(2)]
    psy = [cx.psum("psy%d" % i, [RP, 512]) for i in range(2)]
    psx = [cx.psum("psx%d" % i, [RP, 512]) for i in range(2)]
    psg = [cx.psum("psg%d" % i, [RP, 512]) for i in range(4)]
    ysb = [cx.sbuf("ysb%d" % i, [RP, 512], F32) for i in range(2)]
    gu = [cx.sbuf("gu%d" % i, [RP, 512], F32) for i in range(2)]
    yg = [cx.sbuf("yg%d" % i, [RP, 512], F32) for i in range(4)]
    xc = [cx.sbuf("xc%d" % i, [RP, 512], F32) for i in range(4)]
    xcb = [cx.sbuf("xcb%d" % i, [RP, 512], BF16) for i in range(4)]
    rr = [cx.sbuf("rr%d" % i, [RP, 512], F32) for i in range(2)]
    ii = [cx.sbuf("ii%d" % i, [RP, 512], F32) for i in range(2)]
    aa = [cx.sbuf("aa%d" % i, [RP, 512], F32) for i in range(2)]
    a2 = [cx.sbuf("a2%d" % i, [RP, 512], F32) for i in range(2)]
    hs = [cx.sbuf("hs%d" % i, [RP, 512], F32) for i in range(2)]
    ob = [cx.sbuf("ob%d" % i, [RP, RNC, 512], BF16) for i in range(2)]
    hTd = hT.rearrange("(k p) t -> p k t", p=128)
    oTd = oT.rearrange("(c p) t -> p c t", p=RP)
    n1 = 0
    n2 = 0
    for tb in range(S // 512):
        hb = hTb[tb % 2]
        cx.dma("sp" if tb % 2 == 0 else "act", hb[:], hTd[:, :, tb * 512:(tb + 1) * 512], own=hb, writes=[hb])
        o = ob[tb % 2]
        for n in range(4):
            for half in range(2):
                ck = 2 * n + half
                sl = n1 % 2
                s4 = n1 % 4
                n1 += 1
                py, px = psy[sl], psx[sl]
                for k in range(16):
                    cx.op("pe", lambda e: e.matmul(py[:], lhsT=Wy[:, k, ck * RP:(ck + 1) * RP], rhs=hb[:, k, :],
                                                   start=(k == 0), stop=(k == 15)), reads=[Wy, hb], writes=[py])
                for k in range(16):
                    cx.op("pe", lambda e: e.matmul(px[:], lhsT=Wx[:, k, ck * RP:(ck + 1) * RP], rhs=hb[:, k, :],
                                                   start=(k == 0), stop=(k == 15)), reads=[Wx, hb], writes=[px])
                y_, u_, g_ = ysb[sl], gu[sl], yg[s4]
                cx.op("act", lambda e: e.copy(out=y_[:], in_=py[:]), reads=[py], writes=[y_])
                cx.op("pool", lambda e: e.tensor_tensor(out=u_[:], in0=y_[:], in1=y_[:], op=ALU.mult), reads=[y_], writes=[u_])
                cx.op("pool", lambda e: e.tensor_scalar(out=u_[:], in0=u_[:], scalar1=0.044715, scalar2=1.0,
                                                        op0=ALU.mult, op1=ALU.add), reads=[u_], writes=[u_])
                cx.op("pool", lambda e: e.tensor_tensor(out=u_[:], in0=u_[:], in1=y_[:], op=ALU.mult), reads=[u_, y_], writes=[u_])
                cx.op("act", lambda e: e.activation(out=u_[:], in_=u_[:], func=AF.Sigmoid, scale=GELU_C), reads=[u_], writes=[u_])
                cx.op("pool", lambda e: e.tensor_tensor(out=g_[:], in0=u_[:], in1=y_[:], op=ALU.mult), reads=[u_, y_], writes=[g_])
                cx.op("act", lambda e: e.copy(out=XR[:, ck, 3:515], in_=px[:]), reads=[px], writes=[XR])
                x_ = xc[s4]
                cx.op("dve", lambda e: e.tensor_scalar(out=x_[:], in0=XR[:, ck, 0:512], scalar1=cw[:, ck, 0:1],
                                                       scalar2=cb[:, ck:ck + 1], op0=ALU.mult, op1=ALU.add),
                      reads=[XR, cw, cb], writes=[x_])
                for j in range(1, 4):
                    cx.op("dve", lambda e: e.scalar_tensor_tensor(out=x_[:], in0=XR[:, ck, j:j + 512], scalar=cw[:, ck, j:j + 1],
                                                                  in1=x_[:], op0=ALU.mult, op1=ALU.add),
                          reads=[XR, cw, x_], writes=[x_])
                cx.op("act", lambda e: e.copy(out=xcb[s4][:], in_=x_[:]), reads=[x_], writes=[xcb[s4]])
                cx.op("dve", lambda e: e.tensor_copy(out=XR[:, ck, 0:3], in_=XR[:, ck, 512:515]), reads=[XR], writes=[XR])
            base4 = (n1 - 2) % 4
            for dc in range(2):
                ck = 2 * n + dc
                s4 = (base4 + dc) % 4
                pr, pi = psg[(n2 * 2) % 4], psg[(n2 * 2 + 1) % 4]
                sl = n2 % 2
                n2 += 1
                for g, pp in ((0, pr), (1, pi)):
                    for cc in range(2):
                        cx.op("pe", lambda e: e.matmul(pp[:], lhsT=WG[:, (g * 4 + n) * 2 + cc, dc * RP:(dc + 1) * RP],
                                                       rhs=xcb[(base4 + cc) % 4][:], start=(cc == 0), stop=(cc == 1)),
                              reads=[WG, xcb[(base4 + cc) % 4]], writes=[pp])
                r_, i_, a_, q_, h_ = rr[sl], ii[sl], aa[sl], a2[sl], hs[sl]
                x_ = xc[s4]
                cx.op("act", lambda e: e.activation(out=r_[:], in_=pr[:], func=AF.Sigmoid, bias=bg[:, 0, ck:ck + 1]),
                      reads=[pr, bg], writes=[r_])
                cx.op("act", lambda e: e.activation(out=i_[:], in_=pi[:], func=AF.Sigmoid, bias=bg[:, 1, ck:ck + 1]),
                      reads=[pi, bg], writes=[i_])
                cx.op("act", lambda e: e.activation(out=a_[:], in_=r_[:], func=AF.Exp, scale=cneg[:, ck:ck + 1]),
                      reads=[r_, cneg], writes=[a_])
                cx.op("act", lambda e: e.activation(out=q_[:], in_=r_[:], func=AF.Exp, scale=cneg2[:, ck:ck + 1]),
                      reads=[r_, cneg2], writes=[q_])
                cx.op("act", lambda e: e.activation(out=q_[:], in_=q_[:], func=AF.Sqrt, scale=-1.0, bias=1.0),
                      reads=[q_], writes=[q_])
                cx.op("dve", lambda e: e.tensor_tensor(out=i_[:], in0=i_[:], in1=x_[:], op=ALU.mult), reads=[i_, x_], writes=[i_])
                cx.op("dve", lambda e: e.tensor_tensor(out=i_[:], in0=i_[:], in1=q_[:], op=ALU.mult), reads=[i_, q_], writes=[i_])
                cx.op("dve", lambda e: e.tensor_tensor_scan(out=h_[:], data0=a_[:], data1=i_[:], initial=ST[:, ck:ck + 1],
                                                            op0=ALU.mult, op1=ALU.add), reads=[a_, i_, ST], writes=[h_])
                cx.op("dve", lambda e: e.tensor_copy(out=ST[:, ck:ck + 1], in_=h_[:, 511:512]), reads=[h_], writes=[ST])
                cx.op("dve", lambda e: e.tensor_tensor(out=o[:, ck, :], in0=h_[:], in1=yg[s4][:], op=ALU.mult),
                      reads=[h_, yg[s4]], writes=[o])
        cx.dma("act" if tb % 2 == 0 else "sp", oTd[:, :, tb * 512:(tb + 1) * 512], o[:], own=o, reads=[o])
    return cx.finish()


def make_perm(cx, ident):
    pm = cx.sbuf("permm", [128, 128], BF16)
    cx.op("dve", lambda e: e.tensor_copy(out=pm[:, 0:64], in_=ident[:, 64:128]), reads=[ident], writes=[pm])
    cx.op("dve", lambda e: e.tensor_copy(out=pm[:, 64:128], in_=ident[:, 0:64]), reads=[ident], writes=[pm])
    return pm


def proj_layout(spec):
    nfm = sum(n for k, n in spec if k in ("plain", "rope", "both"))
    nfr = sum(n for k, n in spec if k == "both")
    ntm = sum(n for k, n in spec if k == "tm")
    ngt = sum(n for k, n in spec if k == "gate")
    return nfm, nfr, ntm, ngt


def build_proj(spec):
    NW = sum(n for k, n in spec)
    nfm, nfr, ntm, ngt = proj_layout(spec)
    cx = Ctx()
    x = cx.din("x", [TPC, D], F32)
    gain = cx.din("gain", [D], F32)
    scale = cx.din("scale", [D], F32)
    shift = cx.din("shift", [D], F32)
    w = cx.din("w", [D, NW], F32)
    ctab = cx.din("ctab", [128, TPC], F32)
    stab = cx.din("stab", [128, TPC], F32)
    fmT = cx.dout("fmT", [nfm, TPC], BF16)
    fmR = cx.dout("fmR", [nfr, TPC], BF16) if nfr else None
    tm = cx.dout("tm", [TPC, ntm], BF16) if ntm else None
    gt = cx.dout("gt", [TPC, ngt], F32) if ngt else None

    ident = make_ident(cx)
    pm = make_perm(cx, ident)
    nm = NormMod(cx, ident, "j")
    nm.load_params(gain, scale, shift)
    CT = cx.sbuf("CT", [128, TPC], F32)
    STb = cx.sbuf("STb", [128, TPC], F32)
    cx.dma("act", CT[:], ctab, own=CT, writes=[CT])
    cx.dma("act", STb[:], stab, own=STb, writes=[STb])
    hTa = cx.sbuf("hTa", [128, 16, TPC], BF16)
    xt = [cx.sbuf("xt%d" % i, [128, D], F32) for i in range(2)]
    for i in range(TPC // 128):
        xb = xt[i % 2]
        cx.dma("sp", xb[:], x[i * 128:(i + 1) * 128, :], own=xb, writes=[xb])
        nm.apply_T(xb, xb[:], hTa, lambda g: hTa[:, 4 * g:4 * g + 4, i * 128:(i + 1) * 128])
    NSL = 2
    slabs = [cx.sbuf("pslab%d" % i, [128, 16, 512], BF16) for i in range(NSL)]
    psA = [cx.psum("ppA%d" % i, [128, 512]) for i in range(2)]
    psR = [cx.psum("ppR%d" % i, [128, 512]) for i in range(2)]
    tb16 = [cx.sbuf("tb16%d" % i, [128, 512], BF16) for i in range(3)]
    t1 = [cx.sbuf("t1%d" % i, [128, 512], F32) for i in range(2)]
    t2 = [cx.sbuf("t2%d" % i, [128, 512], F32) for i in range(2)]
    ob = [cx.sbuf("ob%d" % i, [128, 512], BF16) for i in range(3)]
    og = [cx.sbuf("og%d" % i, [128, 64], F32) for i in range(2)]
    c = {"a": 0, "r": 0, "t": 0, "o": 0, "g": 0, "d": 0}

    def dq():
        c["d"] += 1
        return "sp" if c["d"] % 2 else "act"

    col = 0
    rfm = 0
    rfr = 0
    ctm = 0
    cgt = 0
    for si, (kind, ncols) in enumerate(spec):
        sl = slabs[si % NSL]
        cx.dma("pool", sl[:, :, 0:ncols], w[:, col:col + ncols].rearrange("(k p) n -> p k n", p=128), own=sl, writes=[sl])
        if kind in ("plain", "rope", "both"):
            for cc in range(ncols // 128):
                for tb in range(TPC // 512):
                    pa = psA[c["a"] % 2]
                    c["a"] += 1
                    for k in range(16):
                        cx.op("pe", lambda e: e.matmul(pa[:], lhsT=sl[:, k, cc * 128:(cc + 1) * 128], rhs=hTa[:, k, tb * 512:(tb + 1) * 512],
                                                       start=(k == 0), stop=(k == 15)), reads=[sl, hTa], writes=[pa])
                    tb_ = tb16[c["t"] % 3]
                    c["t"] += 1
                    cx.op("act", lambda e: e.copy(out=tb_[:], in_=pa[:]), reads=[pa], writes=[tb_])
                    if kind in ("plain", "both"):
                        cx.dma(dq(), fmT[rfm + cc * 128:rfm + (cc + 1) * 128, tb * 512:(tb + 1) * 512], tb_[:], own=tb_, reads=[tb_])
                    if kind in ("rope", "both"):
                        pr = psR[c["r"] % 2]
                        a1, a2 = t1[c["r"] % 2], t2[c["r"] % 2]
                        c["r"] += 1
                        cx.op("pe", lambda e: e.matmul(pr[:], lhsT=pm[:], rhs=tb_[:], start=True, stop=True), reads=[pm, tb_], writes=[pr])
                        cx.op("dve", lambda e: e.tensor_tensor(out=a1[:], in0=pa[:], in1=CT[:, tb * 512:(tb + 1) * 512], op=ALU.mult),
                              reads=[pa, CT], writes=[a1])
                        cx.op("dve", lambda e: e.tensor_tensor(out=a2[:], in0=pr[:], in1=STb[:, tb * 512:(tb + 1) * 512], op=ALU.mult),
                              reads=[pr, STb], writes=[a2])
                        o_ = ob[c["o"] % 3]
                        c["o"] += 1
                        cx.op("dve", lambda e: e.tensor_tensor(out=o_[:], in0=a1[:], in1=a2[:], op=ALU.add), reads=[a1, a2], writes=[o_])
                        dst = fmR if kind == "both" else fmT
                        r0 = rfr if kind == "both" else rfm
                        cx.dma(dq(), dst[r0 + cc * 128:r0 + (cc + 1) * 128, tb * 512:(tb + 1) * 512], o_[:], own=o_, reads=[o_])
            rfm += ncols
            if kind == "both":
                rfr += ncols
        else:
            for ti in range(TPC // 128):
                pa = psA[c["a"] % 2]
                c["a"] += 1
                for k in range(16):
                    cx.op("pe", lambda e: e.matmul(pa[:, 0:ncols], lhsT=hTa[:, k, ti * 128:(ti + 1) * 128], rhs=sl[:, k, 0:ncols],
                                                   start=(k == 0), stop=(k == 15)), reads=[sl, hTa], writes=[pa])
                if kind == "tm":
                    o_ = ob[c["o"] % 3]
                    c["o"] += 1
                    cx.op("act", lambda e: e.copy(out=o_[:, 0:ncols], in_=pa[:, 0:ncols]), reads=[pa], writes=[o_])
                    cx.dma(dq(), tm[ti * 128:(ti + 1) * 128, ctm:ctm + ncols], o_[:, 0:ncols], own=o_, reads=[o_])
                else:
                    o_ = og[c["g"] % 2]
                    c["g"] += 1
                    cx.op("act", lambda e: e.activation(out=o_[:, 0:ncols], in_=pa[:, 0:ncols], func=AF.Sigmoid), reads=[pa], writes=[o_])
                    cx.dma(dq(), gt[ti * 128:(ti + 1) * 128, cgt:cgt + ncols], o_[:, 0:ncols], own=o_, reads=[o_])
            if kind == "tm":
                ctm += ncols
            else:
                cgt += ncols
        col += ncols
    return cx.finish()


def rope_tabs():
    inv = (10000.0 ** (-np.arange(0, 128, 2, dtype=np.float32) / 128)).astype(np.float32)
    ang = np.arange(S, dtype=np.float32)[:, None] * inv[None, :]
    cos, sin = np.cos(ang).astype(np.float32), np.sin(ang).astype(np.float32)
    ct = np.concatenate([cos, cos], axis=1).T
    st = np.concatenate([-sin, sin], axis=1).T
    return np.ascontiguousarray(ct), np.ascontiguousarray(st)


def run_proj(nc, spec, x, gain, scale, shift, w):
    ct, st = rope_tabs()
    maps = []
    for c in range(NCORES):
        b, q = divmod(c, 4)
        ts = slice(q * TPC, (q + 1) * TPC)
        maps.append({"x": np.ascontiguousarray(x[b, ts]), "gain": gain, "scale": np.ascontiguousarray(scale[b]),
                     "shift": np.ascontiguousarray(shift[b]), "w": w,
                     "ctab": np.ascontiguousarray(ct[:, ts]), "stab": np.ascontiguousarray(st[:, ts])})
    res = run_bass_kernel_spmd(nc, maps, core_ids=list(range(NCORES))).results
    out = {}
    nfm, nfr, ntm, ngt = proj_layout(spec)
    out["fmT"] = np.stack([np.concatenate([res[b * 4 + q]["fmT"] for q in range(4)], axis=1) for b in range(B)])
    if nfr:
        out["fmR"] = np.stack([np.concatenate([res[b * 4 + q]["fmR"] for q in range(4)], axis=1) for b in range(B)])
    if ntm:
        out["tm"] = np.stack([np.concatenate([res[b * 4 + q]["tm"] for q in range(4)], axis=0) for b in range(B)])
    if ngt:
        out["gt"] = np.stack([np.concatenate([res[b * 4 + q]["gt"] for q in range(4)], axis=0) for b in range(B)])
    return out


DIL_SPEC = [(k, 512) for p in range(3) for k in ("rope",) * 8 + ("tm",) * 4]
NSA_SPEC = [("both", 512)] * 4 + [("plain", 512), ("plain", 512), ("rope", 512), ("tm", 512), ("rope", 512), ("tm", 512), ("gate", 48)]


DILS = (1, 4, 16)


def make_ident_f32(cx):
    ident = cx.sbuf("identf", [128, 128], F32)
    cx.op("pool", lambda e: e.memset(ident[:], 0.0), writes=[ident])
    cx.op("pool", lambda e: e.affine_select(out=ident[:], in_=ident[:], pattern=[[-1, 128]],
                                            compare_op=ALU.not_equal, fill=1.0, base=0,
                                            channel_multiplier=1), reads=[ident], writes=[ident])
    return ident


def build_dil_attn():
    cx = Ctx()
    qT = cx.din("qT", [4, 3, 128, S], BF16)
    kT = cx.din("kT", [4, 3, 128, S], BF16)
    v = cx.din("v", [4, 3, S, 128], BF16)
    masks = cx.din("masks", [128, 256], BF16)
    oT = cx.dout("oT", [512, S], BF16)
    identf = make_ident_f32(cx)
    MK = cx.sbuf("MK", [128, 256], BF16)
    cx.dma("sp", MK[:], masks, own=MK, writes=[MK])
    ones_b = cx.sbuf("ones_b", [128, 1], BF16)
    ones_f = cx.sbuf("ones_f", [1, 128], F32)
    cx.op("dve", lambda e: e.memset(ones_b[:], 1.0), writes=[ones_b])
    cx.op("dve", lambda e: e.memset(ones_f[:], 1.0), writes=[ones_f])
    ACC = cx.sbuf("ACC", [128, S], F32)
    L = cx.sbuf("L", [1, S], F32)
    Kb = [cx.sbuf("Kb%d" % i, [128, S], BF16) for i in range(2)]
    Qb = [cx.sbuf("Qb%d" % i, [128, S], BF16) for i in range(2)]
    Vb = [cx.sbuf("Vb%d" % i, [128, 64, 128], BF16) for i in range(2)]
    ps = [cx.psum("dps%d" % i, [128, 256]) for i in range(2)]
    po = [cx.psum("dpo%d" % i, [128, 128]) for i in range(2)]
    pl = cx.psum("dpl", [1, 256])
    pt = [cx.psum("dpt%d" % i, [128, 128]) for i in range(2)]
    pbk = cx.psum("dpbk", [128, 512])
    Pb = [cx.sbuf("Pb%d" % i, [128, 256], BF16) for i in range(3)]
    osb = [cx.sbuf("osb%d" % i, [128, 128], F32) for i in range(2)]
    otb = [cx.sbuf("otb%d" % i, [128, 512], BF16) for i in range(2)]
    n = 0
    it = 0
    for hh in range(4):
        for p in range(3):
            d = DILS[p]
            K_, Q_, V_ = Kb[n % 2], Qb[n % 2], Vb[n % 2]
            n += 1
            cx.dma("sp", K_[:], kT[hh, p], own=K_, writes=[K_])
            cx.dma("act", Q_[:], qT[hh, p], own=Q_, writes=[Q_])
            cx.dma("sp", V_[:], v[hh, p].rearrange("(t p) d -> p t d", p=128), own=V_, writes=[V_])
            for tau in range(64):
                J, r = divmod(tau, d)
                has_prev = tau >= d
                lo = 0 if has_prev else 128
                s_ = ps[it % 2]
                o_ = po[it % 2]
                t_ = pt[it % 2]
                P_ = Pb[it % 3]
                ob_ = osb[it % 2]
                it += 1
                qs = Q_[:, tau * 128:(tau + 1) * 128]
                if has_prev:
                    cx.op("pe", lambda e: e.matmul(s_[:, 0:128], lhsT=K_[:, (tau - d) * 128:(tau - d + 1) * 128], rhs=qs,
                                                   start=True, stop=True), reads=[K_, Q_], writes=[s_])
                cx.op("pe", lambda e: e.matmul(s_[:, 128:256], lhsT=K_[:, tau * 128:(tau + 1) * 128], rhs=qs,
                                               start=True, stop=True), reads=[K_, Q_], writes=[s_])
                cx.op("act", lambda e: e.activation(out=P_[:, lo:256], in_=s_[:, lo:256], func=AF.Exp, scale=SCALE),
                      reads=[s_], writes=[P_])
                cx.op("dve", lambda e: e.tensor_tensor(out=P_[:, lo:256], in0=P_[:, lo:256], in1=MK[:, lo:256], op=ALU.mult),
                      reads=[P_, MK], writes=[P_])
                if has_prev:
                    cx.op("pe", lambda e: e.matmul(o_[:], lhsT=P_[:, 0:128], rhs=V_[:, tau - d, :], start=True, stop=False),
                          reads=[P_, V_], writes=[o_])
                cx.op("pe", lambda e: e.matmul(o_[:], lhsT=P_[:, 128:256], rhs=V_[:, tau, :], start=(not has_prev), stop=True),
                      reads=[P_, V_], writes=[o_])
                cx.op("pe", lambda e: e.matmul(pl[:, lo:256], lhsT=ones_b[:], rhs=P_[:, lo:256], start=True, stop=True),
                      reads=[P_, ones_b], writes=[pl])
                cx.op("act", lambda e: e.copy(out=ob_[:], in_=o_[:]), reads=[o_], writes=[ob_])
                cx.op("pe", lambda e: e.transpose(out=t_[:], in_=ob_[:], identity=identf[:]), reads=[ob_, identf], writes=[t_])
                c0 = 128 * d * J + r
                c1 = 128 * d * (J + 1)
                acc = ACC[:, c0:c1:d]
                lac = L[:, c0:c1:d]
                if p == 0:
                    cx.op("dve", lambda e: e.tensor_copy(out=acc, in_=t_[:]), reads=[t_], writes=[ACC])
                    cx.op("dve", lambda e: e.tensor_copy(out=lac, in_=pl[:, 128:256]), reads=[pl], writes=[L])
                else:
                    cx.op("dve", lambda e: e.tensor_tensor(out=acc, in0=acc, in1=t_[:], op=ALU.add), reads=[t_, ACC], writes=[ACC])
                    cx.op("dve", lambda e: e.tensor_tensor(out=lac, in0=lac, in1=pl[:, 128:256], op=ALU.add), reads=[pl, L], writes=[L])
                if has_prev:
                    cx.op("dve", lambda e: e.tensor_tensor(out=lac, in0=lac, in1=pl[:, 0:128], op=ALU.add), reads=[pl, L], writes=[L])
        cx.op("dve", lambda e: e.reciprocal(out=L[:], in_=L[:]), reads=[L], writes=[L])
        for blk in range(S // 512):
            cx.op("pe", lambda e: e.matmul(pbk[:], lhsT=ones_f[:], rhs=L[:, blk * 512:(blk + 1) * 512], start=True, stop=True),
                  reads=[ones_f, L], writes=[pbk])
            ot_ = otb[blk % 2]
            cx.op("dve", lambda e: e.tensor_tensor(out=ot_[:], in0=pbk[:], in1=ACC[:, blk * 512:(blk + 1) * 512], op=ALU.mult),
                  reads=[pbk, ACC], writes=[ot_])
            cx.dma("sp" if blk % 2 else "act", oT[hh * 128:(hh + 1) * 128, blk * 512:(blk + 1) * 512], ot_[:], own=ot_, reads=[ot_])
    return cx.finish()


def dil_perm(d):
    j = np.arange(S)
    J = j // (128 * d)
    r = (j % (128 * d)) // 128
    i = j % 128
    return 128 * d * J + r + d * i


def dil_masks():
    p = np.arange(128)[:, None]
    f = np.arange(128)[None, :]
    return np.concatenate([(p >= f), (p <= f)], axis=1).astype(NPBF)


def dil_attn_maps(fmT, tm):
    perms = [dil_perm(d) for d in DILS]
    mk = dil_masks()
    maps = []
    for c in range(NCORES):
        b, hq = divmod(c, 4)
        q = np.empty((4, 3, 128, S), NPBF)
        k = np.empty((4, 3, 128, S), NPBF)
        vv = np.empty((4, 3, S, 128), NPBF)
        for hh in range(4):
            h = 4 * hq + hh
            for p in range(3):
                rq = ((p * 2 + 0) * 16 + h) * 128
                rk = ((p * 2 + 1) * 16 + h) * 128
                q[hh, p] = fmT[b, rq:rq + 128][:, perms[p]]
                k[hh, p] = fmT[b, rk:rk + 128][:, perms[p]]
                cv = (p * 16 + h) * 128
                vv[hh, p] = tm[b][perms[p], cv:cv + 128]
        maps.append({"qT": q, "kT": k, "v": vv, "masks": mk})
    return maps


NEGBIG = 30000.0


def build_nsa_attn():
    cx = Ctx()
    qP = cx.din("qP", [4, 128, S], BF16)
    qR = cx.din("qR", [4, 128, S], BF16)
    kcmpT = cx.din("kcmpT", [128, S], BF16)
    vcmpT = cx.din("vcmpT", [128, S], BF16)
    kselT = cx.din("kselT", [128, S], BF16)
    kwinT = cx.din("kwinT", [128, S], BF16)
    vsel = cx.din("vsel", [S, 128], BF16)
    vwin = cx.din("vwin", [S, 128], BF16)
    gates = cx.din("gates", [S, 12], F32)
    peT = cx.din("peT", [2, 128, 32], F32)
    w1 = cx.din("w1", [2, 4096, 512], F32)
    w2 = cx.din("w2", [2, 512, 128], F32)
    cmask = cx.din("cmask", [128, 13, 512], BF16)
    eall = cx.din("eall", [128, S], BF16)
    ovl = cx.din("ovl", [128, 4, 128], BF16)
    bonus = cx.din("bonus", [64, 128, 128], F32)
    oT = cx.dout("oT", [512, S], BF16)

    ident = make_ident(cx)
    KS = cx.sbuf("KS", [128, S], BF16)
    KW = cx.sbuf("KW", [128, S], BF16)
    VS = cx.sbuf("VS", [128, 64, 129], BF16)
    VW = cx.sbuf("VW", [128, 64, 129], BF16)
    cx.dma("sp", KS[:], kselT, own=KS, writes=[KS])
    cx.dma("act", KW[:], kwinT, own=KW, writes=[KW])
    cx.op("dve", lambda e: e.memset(VS[:, :, 128:129], 1.0), writes=[VS])
    cx.op("dve", lambda e: e.memset(VW[:, :, 128:129], 1.0), writes=[VW])
    cx.dma("sp", VS[:, :, 0:128], vsel.rearrange("(t p) d -> p t d", p=128), own=VS, writes=[VS])
    cx.dma("act", VW[:, :, 0:128], vwin.rearrange("(t p) d -> p t d", p=128), own=VW, writes=[VW])
    kcT = cx.sbuf("kcT", [128, 512], BF16)
    VCO = cx.sbuf("VCO", [128, 4, 257], BF16)
    cx.op("dve", lambda e: e.memset(VCO[:, :, 256:257], 1.0), writes=[VCO])
    cx.dma("sp", VCO[:, :, 128:256], ovl, own=VCO, writes=[VCO])

    big = cx.sbuf("big", [128, 16384], BF16)
    W1v = big[:].rearrange("p (j n) -> p j n", j=32)
    cmpT = cx.sbuf("cmpT", [128, S + 32], BF16)
    pe_sb = cx.sbuf("pe_sb", [128, 32], BF16)
    W2 = cx.sbuf("W2", [128, 4, 128], BF16)
    bias_sb = cx.sbuf("bias_sb", [128, 4], F32)
    yb = cx.sbuf("yb", [128, 512], F32)
    ub = cx.sbuf("ub", [128, 512], F32)
    hg = cx.sbuf("hg", [128, 4, 128], BF16)

    pS = [cx.psum("pS%d" % i, [128, 512]) for i in range(2)]
    pO = [cx.psum("pO%d" % i, [128, 257]) for i in range(4)]
    pX = [cx.psum("pX%d" % i, [128, 4, 128], BF16) for i in range(2)]

    cx.op("dve", lambda e: e.memset(cmpT[:, S:S + 32], 0.0), writes=[cmpT])
    for kv in range(2):
        cx.dma("pool", W1v, w1[kv].rearrange("(j d) n -> d j n", d=128), own=big, writes=[big])
        cx.dma("pool", pe_sb[:], peT[kv], own=pe_sb, writes=[pe_sb])
        cx.dma("pool", W2[:], w2[kv].rearrange("(c p) d -> p c d", p=128), own=W2, writes=[W2])
        cx.dma("sp", cmpT[:, 0:S], kcmpT if kv == 0 else vcmpT, own=cmpT, writes=[cmpT])
        pb = pO[0]
        for hc in range(4):
            for j in range(32):
                cx.op("pe", lambda e: e.matmul(pb[:, hc:hc + 1], lhsT=W1v[:, j, hc * 128:(hc + 1) * 128], rhs=pe_sb[:, j:j + 1],
                                               start=(j == 0), stop=(j == 31)), reads=[big, pe_sb], writes=[pb])
        cx.op("dve", lambda e: e.tensor_copy(out=bias_sb[:], in_=pb[:, 0:4]), reads=[pb], writes=[bias_sb])
        for c in range(4):
            ph = pS[c % 2]
            for hc in range(4):
                for j in range(32):
                    cx.op("pe", lambda e: e.matmul(ph[:, hc * 128:(hc + 1) * 128], lhsT=W1v[:, j, hc * 128:(hc + 1) * 128],
                                                   rhs=cmpT[:, 2048 * c + j:2048 * c + j + 2048:16],
                                                   start=(j == 0), stop=(j == 31)), reads=[big, cmpT], writes=[ph])
            for hc in range(4):
                cx.op("act", lambda e: e.activation(out=yb[:, hc * 128:(hc + 1) * 128], in_=ph[:, hc * 128:(hc + 1) * 128],
                                                    func=AF.Identity, bias=bias_sb[:, hc:hc + 1]),
                      reads=[ph, bias_sb], writes=[yb])
            cx.op("dve", lambda e: e.tensor_tensor(out=ub[:], in0=yb[:], in1=yb[:], op=ALU.mult), reads=[yb], writes=[ub])
            cx.op("dve", lambda e: e.tensor_scalar(out=ub[:], in0=ub[:], scalar1=0.044715, scalar2=1.0, op0=ALU.mult, op1=ALU.add),
                  reads=[ub], writes=[ub])
            cx.op("dve", lambda e: e.tensor_tensor(out=ub[:], in0=ub[:], in1=yb[:], op=ALU.mult), reads=[ub, yb], writes=[ub])
            cx.op("act", lambda e: e.activation(out=ub[:], in_=ub[:], func=AF.Sigmoid, scale=GELU_C), reads=[ub], writes=[ub])
            cx.op("dve", lambda e: e.tensor_tensor(out=hg[:].rearrange("p a b -> p (a b)"), in0=ub[:], in1=yb[:], op=ALU.mult),
                  reads=[ub, yb], writes=[hg])
            pk = pO[1 + c % 2]
            for hc in range(4):
                if kv == 0:
                    cx.op("pe", lambda e: e.matmul(pk[:, 0:128], lhsT=W2[:, hc, :], rhs=hg[:, hc, :], start=(hc == 0), stop=(hc == 3)),
                          reads=[W2, hg], writes=[pk])
                else:
                    cx.op("pe", lambda e: e.matmul(pk[:, 0:128], lhsT=hg[:, hc, :], rhs=W2[:, hc, :], start=(hc == 0), stop=(hc == 3)),
                          reads=[W2, hg], writes=[pk])
            if kv == 0:
                cx.op("act", lambda e: e.copy(out=kcT[:, c * 128:(c + 1) * 128], in_=pk[:, 0:128]), reads=[pk], writes=[kcT])
            else:
                cx.op("act", lambda e: e.copy(out=VCO[:, c, 0:128], in_=pk[:, 0:128]), reads=[pk], writes=[VCO])

    EALL = big[:, 0:S]
    CM = big[:, S:S + 13 * 512].rearrange("p (m q) -> p m q", m=13)
    cx.dma("sp", EALL, eall, own=big, writes=[big])
    cx.dma("act", CM, cmask, own=big, writes=[big])
    qPb = [cx.sbuf("qPb%d" % i, [128, 4, 512], BF16) for i in range(2)]
    qRb = [cx.sbuf("qRb%d" % i, [128, 4, 512], BF16) for i in range(2)]
    Gt = [cx.sbuf("Gt%d" % i, [128, 4, 12], F32) for i in range(2)]
    Bt = [cx.sbuf("Bt%d" % i, [128, 4, 128], F32) for i in range(2)]
    Pt = [cx.sbuf("Pt%d" % i, [128, 512], BF16) for i in range(3)]
    OACC = cx.sbuf("OACC", [128, 4, 512], F32)
    OB = cx.sbuf("OB", [128, 512], BF16)
    imp = cx.sbuf("imp", [128, 4, 128], F32)
    sct = cx.sbuf("sct", [128, 128], F32)
    wk = cx.sbuf("wk", [128, 128], F32)
    m8 = cx.sbuf("m8", [128, 8], F32)
    m8b = cx.sbuf("m8b", [128, 8], F32)
    nsf = cx.sbuf("nsf", [128, 128], F32)
    nsb = cx.sbuf("nsb", [128, 128], BF16)
    nsT = cx.sbuf("nsT", [128, 512], BF16)
    rl = [cx.sbuf("rl%d" % i, [128, 1], F32) for i in range(4)]
    scg = [cx.sbuf("scg%d" % i, [128, 1], F32) for i in range(4)]
    oTb = [cx.sbuf("oTb%d" % i, [128, 4, 512], BF16) for i in range(2)]
    cnt = {"s": 0, "p": 0, "x": 0}

    def score_tile(lhs_buf, lhs_ap, q_buf, q_ap, extra=None):
        s_ = pS[cnt["s"] % 2]
        cnt["s"] += 1
        cx.op("pe", lambda e: e.matmul(s_[:], lhsT=lhs_ap, rhs=q_ap, start=True, stop=(extra is None)),
              reads=[lhs_buf, q_buf], writes=[s_])
        if extra is not None:
            cx.op("pe", lambda e: e.matmul(s_[:], lhsT=extra[0], rhs=extra[1], start=False, stop=True),
                  reads=[big, nsT], writes=[s_])
        P_ = Pt[cnt["p"] % 3]
        cnt["p"] += 1
        cx.op("act", lambda e: e.activation(out=P_[:], in_=s_[:], func=AF.Exp, scale=SCALE), reads=[s_], writes=[P_])
        return P_

    def apply_mask(P_, m):
        cx.op("dve", lambda e: e.tensor_tensor(out=P_[:], in0=P_[:], in1=CM[:, m, :], op=ALU.mult), reads=[P_, big], writes=[P_])

    def finalize(qt, r, br, ncol, first, gt_):
        po = pO[qt]
        rl_, sc_ = rl[qt], scg[qt]
        cx.op("dve", lambda e: e.tensor_scalar(out=rl_[:], in0=po[:, ncol:ncol + 1], scalar1=1e-30, scalar2=None, op0=ALU.max),
              reads=[po], writes=[rl_])
        cx.op("dve", lambda e: e.reciprocal(out=rl_[:], in_=rl_[:]), reads=[rl_], writes=[rl_])
        cx.op("dve", lambda e: e.tensor_tensor(out=sc_[:], in0=rl_[:], in1=gt_[:, qt, r * 3 + br:r * 3 + br + 1], op=ALU.mult),
              reads=[rl_, gt_], writes=[sc_])
        dst = OACC[:, qt, r * 128:(r + 1) * 128]
        if first:
            cx.op("dve", lambda e: e.tensor_scalar(out=dst, in0=po[:, 0:128], scalar1=sc_[:, 0:1], scalar2=None, op0=ALU.mult),
                  reads=[po, sc_], writes=[OACC])
        else:
            cx.op("dve", lambda e: e.scalar_tensor_tensor(out=dst, in0=po[:, 0:128], scalar=sc_[:, 0:1], in1=dst,
                                                          op0=ALU.mult, op1=ALU.add), reads=[po, sc_, OACC], writes=[OACC])
        if br == 0:
            if r == 0:
                cx.op("dve", lambda e: e.tensor_scalar(out=imp[:, qt, :], in0=po[:, 128:256], scalar1=rl_[:, 0:1], scalar2=None,
                                                       op0=ALU.mult), reads=[po, rl_], writes=[imp])
            else:
                cx.op("dve", lambda e: e.scalar_tensor_tensor(out=imp[:, qt, :], in0=po[:, 128:256], scalar=rl_[:, 0:1], in1=imp[:, qt, :],
                                                              op0=ALU.mult, op1=ALU.add), reads=[po, rl_, imp], writes=[imp])

    for tb in range(S // 512):
        qp, qr, gt_, bt_ = qPb[tb % 2], qRb[tb % 2], Gt[tb % 2], Bt[tb % 2]
        ts = slice(tb * 512, (tb + 1) * 512)
        cx.dma("sp", qp[:], qP[:, :, ts].rearrange("h d t -> d h t"), own=qp, writes=[qp])
        cx.dma("act", qr[:], qR[:, :, ts].rearrange("h d t -> d h t"), own=qr, writes=[qr])
        cx.dma("sp", gt_[:], gates[ts, :].rearrange("(q p) c -> p q c", p=128), own=gt_, writes=[gt_])
        cx.dma("act", bt_[:], bonus[4 * tb:4 * tb + 4].rearrange("q p j -> p q j"), own=bt_, writes=[bt_])
        nvis = tb // 4 + 1
        for r in range(4):
            for c in range(nvis):
                P_ = score_tile(kcT, kcT[:, c * 128:(c + 1) * 128], qp, qp[:, r, :])
                if tb - 4 * c <= 4:
                    apply_mask(P_, tb - 4 * c)
                for qt in range(4):
                    cx.op("pe", lambda e: e.matmul(pO[qt][:, 0:257], lhsT=P_[:, qt * 128:(qt + 1) * 128], rhs=VCO[:, c, :],
                                                   start=(c == 0), stop=(c == nvis - 1)), reads=[P_, VCO], writes=[pO[qt]])
            for qt in range(4):
                finalize(qt, r, 0, 256, True, gt_)
        for qt in range(4):
            cx.op("dve", lambda e: e.tensor_tensor(out=sct[:], in0=imp[:, qt, :], in1=bt_[:, qt, :], op=ALU.add),
                  reads=[imp, bt_], writes=[sct])
            cx.op("dve", lambda e: e.max(out=m8[:], in_=sct[:]), reads=[sct], writes=[m8])
            cx.op("dve", lambda e: e.match_replace(out=wk[:], in_to_replace=m8[:], in_values=sct[:], imm_value=-1e30),
                  reads=[m8, sct], writes=[wk])
            cx.op("dve", lambda e: e.max(out=m8b[:], in_=wk[:]), reads=[wk], writes=[m8b])
            cx.op("dve", lambda e: e.tensor_scalar(out=nsf[:], in0=sct[:], scalar1=m8b[:, 7:8], scalar2=None, op0=ALU.is_ge),
                  reads=[sct, m8b], writes=[nsf])
            cx.op("dve", lambda e: e.tensor_scalar(out=nsb[:], in0=nsf[:], scalar1=-1.0, scalar2=NEGBIG, op0=ALU.add, op1=ALU.mult),
                  reads=[nsf], writes=[nsb])
            px = pX[cnt["x"] % 2]
            cnt["x"] += 1
            cx.op("pe", lambda e: e.transpose(out=px[:, 0, :], in_=nsb[:], identity=ident[:]), reads=[nsb, ident], writes=[px])
            cx.op("act", lambda e: e.copy(out=nsT[:, qt * 128:(qt + 1) * 128], in_=px[:, 0, :]), reads=[px], writes=[nsT])
        for r in range(4):
            for kt in range(4 * tb + 4):
                P_ = score_tile(KS, KS[:, kt * 128:(kt + 1) * 128], qr, qr[:, r, :],
                                extra=(EALL[:, kt * 128:(kt + 1) * 128], nsT[:]))
                i = kt - 4 * tb
                if i >= 0:
                    apply_mask(P_, 5 + i)
                for qt in range(max(0, i), 4):
                    cx.op("pe", lambda e: e.matmul(pO[qt][:, 0:129], lhsT=P_[:, qt * 128:(qt + 1) * 128], rhs=VS[:, kt, :],
                                                   start=(kt == 0), stop=(kt == 4 * tb + qt)), reads=[P_, VS], writes=[pO[qt]])
            for qt in range(4):
                finalize(qt, r, 1, 128, False, gt_)
        for r in range(4):
            for i in range(8):
                kt = 4 * tb - 4 + i
                if kt < 0:
                    continue
                P_ = score_tile(KW, KW[:, kt * 128:(kt + 1) * 128], qr, qr[:, r, :])
                apply_mask(P_, 9 + i if i < 4 else 5 + (i - 4))
                qts = range(0, i + 1) if i < 4 else range(i - 4, 4)
                for qt in qts:
                    st = (i == qt) if tb >= 1 else (i == 4)
                    cx.op("pe", lambda e: e.matmul(pO[qt][:, 0:129], lhsT=P_[:, qt * 128:(qt + 1) * 128], rhs=VW[:, kt, :],
                                                   start=st, stop=(i == 4 + qt)), reads=[P_, VW], writes=[pO[qt]])
            for qt in range(4):
                finalize(qt, r, 2, 128, False, gt_)
        ot_ = oTb[tb % 2]
        for qt in range(4):
            cx.op("act", lambda e: e.copy(out=OB[:], in_=OACC[:, qt, :]), reads=[OACC], writes=[OB])
            px = pX[cnt["x"] % 2]
            cnt["x"] += 1
            for r in range(4):
                cx.op("pe", lambda e: e.transpose(out=px[:, r, :], in_=OB[:, r * 128:(r + 1) * 128], identity=ident[:]),
                      reads=[OB, ident], writes=[px])
            cx.op("act", lambda e: e.copy(out=ot_[:, :, qt * 128:(qt + 1) * 128], in_=px[:]), reads=[px], writes=[ot_])
        cx.dma("sp", oT[:, ts].rearrange("(r d) t -> d r t", d=128), ot_[:], own=ot_, reads=[ot_])
    return cx.finish()


def nsa_consts():
    p = np.arange(128)[:, None]
    f = np.arange(512)[None, :]
    cm = np.zeros((128, 13, 512), np.float32)
    for dl in range(5):
        cm[:, dl, :] = (f - 16 * p + 512 * dl - 31 >= 0)
    for i in range(4):
        cm[:, 5 + i, :] = (f - 128 * i - p >= 0)
        cm[:, 9 + i, :] = (p - f + 128 * i - 1 >= 0)
    key = np.arange(S)[None, :]
    j = np.arange(128)[:, None]
    eall = (key // 64 == j).astype(np.float32)
    n = np.arange(512)
    cs = 16 * n
    ss = 64 * np.arange(128)
    ov = ((cs[:, None] < ss[None, :] + 64) & (cs[:, None] + 32 > ss[None, :])).astype(np.float32)
    ov[511] = 0
    ovl = ov.reshape(4, 128, 128).transpose(1, 0, 2)
    t = np.arange(S)[:, None]
    jj = np.arange(128)[None, :]
    cur = t // 64
    avail = jj <= cur
    forced = (jj == 0) | (jj == cur) | (jj == cur - 1)
    bon = np.where(avail, np.where(forced, 100.0, 0.0), -1e4).astype(np.float32).reshape(64, 128, 128)
    return {"cmask": cm.astype(NPBF), "eall": eall.astype(NPBF), "ovl": np.ascontiguousarray(ovl).astype(NPBF), "bonus": bon}


def nsa_attn_maps(pr, cmp_pe, cmp_w1, cmp_w2):
    cst = nsa_consts()
    peT = np.ascontiguousarray(cmp_pe.transpose(0, 2, 1))
    maps = []
    for c in range(NCORES):
        b, g = divmod(c, 4)
        fm, fr, tmv, gt = pr["fmT"][b], pr["fmR"][b], pr["tm"][b], pr["gt"][b]
        gs = slice(g * 128, (g + 1) * 128)
        m = {"qP": np.ascontiguousarray(fm[g * 512:(g + 1) * 512].reshape(4, 128, S)),
             "qR": np.ascontiguousarray(fr[g * 512:(g + 1) * 512].reshape(4, 128, S)),
             "kcmpT": np.ascontiguousarray(fm[2048:2560][gs]), "vcmpT": np.ascontiguousarray(fm[2560:3072][gs]),
             "kselT": np.ascontiguousarray(fm[3072:3584][gs]), "kwinT": np.ascontiguousarray(fm[3584:4096][gs]),
             "vsel": np.ascontiguousarray(tmv[:, 0:512][:, gs]), "vwin": np.ascontiguousarray(tmv[:, 512:1024][:, gs]),
             "gates": np.ascontiguousarray(gt[:, g * 12:(g + 1) * 12]),
             "peT": peT, "w1": cmp_w1, "w2": cmp_w2}
        m.update(cst)
        maps.append(m)
    return maps


_PROGS = {}


def _prog(key, fn, *args):
    if key not in _PROGS:
        _PROGS[key] = fn(*args)
    return _PROGS[key]


def _launch(nc, maps):
    return run_bass_kernel_spmd(nc, maps, core_ids=list(range(NCORES))).results


def run_pre(x, gain, scale, shift):
    nc = _prog("pre", build_pre)
    maps = []
    for c in range(NCORES):
        b, q = divmod(c, 4)
        maps.append({"x": np.ascontiguousarray(x[b, q * TPC:(q + 1) * TPC]), "gain": gain,
                     "scale": np.ascontiguousarray(scale[b]), "shift": np.ascontiguousarray(shift[b])})
    res = _launch(nc, maps)
    return [np.concatenate([res[b * 4 + q]["hT"] for q in range(4)], axis=1) for b in range(B)]


def run_rglru(hTs, inp, p):
    nc = _prog("rglru", build_rglru)
    w_in = inp[p + "w_in"]
    conv_w, conv_b = inp[p + "conv_w"], inp[p + "conv_b"]
    w_gate, b_gate, lam = inp[p + "w_gate"], inp[p + "b_gate"], inp[p + "lambda"]
    maps = []
    for c in range(NCORES):
        b, q = divmod(c, 4)
        ch = slice(q * RCH, (q + 1) * RCH)
        wgl = w_gate[:, q * 4:(q + 1) * 4].reshape(2, 4, 2, RP, 168).transpose(3, 0, 1, 2, 4).reshape(RP, 16, 168)
        maps.append({"hT": hTs[b], "w_y": np.ascontiguousarray(w_in[:, ch]),
                     "w_x": np.ascontiguousarray(w_in[:, 2688 + q * RCH:2688 + (q + 1) * RCH]),
                     "convw": np.ascontiguousarray(conv_w[:, ch].reshape(4, RNC, RP).transpose(2, 1, 0)),
                     "convb": np.ascontiguousarray(conv_b[ch].reshape(RNC, RP).T),
                     "bgate": np.ascontiguousarray(b_gate[:, ch].reshape(2, RNC, RP).transpose(2, 0, 1)),
                     "lam": np.ascontiguousarray(lam[ch].reshape(RNC, RP).T),
                     "wg": np.ascontiguousarray(wgl)})
    res = _launch(nc, maps)
    return [np.concatenate([res[b * 4 + q]["oT"] for q in range(4)], axis=0) for b in range(B)]


def run_post(x, oTs, KO, last, inp, p, g1, sc2, sh2, g2):
    nc = _prog(("post", KO, last), build_post, KO, last)
    maps = []
    for c in range(NCORES):
        b, q = divmod(c, 4)
        ts = slice(q * TPC, (q + 1) * TPC)
        m = {"x": np.ascontiguousarray(x[b, ts]), "oT": np.ascontiguousarray(oTs[b][:, ts]), "w_out": inp[p + "w_out"],
             "g1": np.ascontiguousarray(g1[b]), "gain2": inp[p + "norm2"], "sc2": np.ascontiguousarray(sc2[b]),
             "sh2": np.ascontiguousarray(sh2[b]), "g2": np.ascontiguousarray(g2[b]),
             "ff1": inp[p + "w_ff1"], "ff2": inp[p + "w_ff2"]}
        if last:
            m["normf"] = inp["norm_f"]
        maps.append(m)
    res = _launch(nc, maps)
    return np.stack([np.concatenate([res[b * 4 + q]["xo"] for q in range(4)], axis=0) for b in range(B)])


def kernel(**inp):
    inp = {k: np.asarray(v) for k, v in inp.items()}
    mod = run_ada(inp)
    x = inp["x"]
    for l in range(4):
        p = "l%d_" % l
        sh1, sc1, g1, sh2, sc2, g2 = [mod[l][:, i * D:(i + 1) * D] for i in range(6)]
        kind = l % 3
        if kind == 0:
            nc = _prog("proj_nsa", build_proj, NSA_SPEC)
            pr = run_proj(nc, NSA_SPEC, x, inp[p + "norm1"], sc1, sh1, inp[p + "w_in"])
            res = _launch(_prog("nsa_attn", build_nsa_attn),
                          nsa_attn_maps(pr, inp[p + "cmp_pe"], inp[p + "cmp_w1"], inp[p + "cmp_w2"]))
            oTs = [np.concatenate([res[b * 4 + g]["oT"] for g in range(4)], axis=0) for b in range(B)]
            KO = 2048
        elif kind == 1:
            nc = _prog("proj_dil", build_proj, DIL_SPEC)
            pr = run_proj(nc, DIL_SPEC, x, inp[p + "norm1"], sc1, sh1, inp[p + "w_in"])
            res = _launch(_prog("dil_attn", build_dil_attn), dil_attn_maps(pr["fmT"], pr["tm"]))
            oTs = [np.concatenate([res[b * 4 + g]["oT"] for g in range(4)], axis=0) for b in range(B)]
            KO = 2048
        else:
            hTs = run_pre(x, inp[p + "norm1"], sc1, sh1)
            oTs = run_rglru(hTs, inp, p)
            KO = 2688
        x = run_post(x, oTs, KO, l == 3, inp, p, g1, sc2, sh2, g2)
    return x.astype(np.float32)
```
